# Optimizing a Trainium2 kernel written in Bass

```python
import math
import jax, jax.numpy as jnp
from jax import lax
import numpy as np

D_MODEL = 4096
BATCH = 1
SEQ = 8192
DEPTH = 4
DEC_BATCH = 8
DEC_SEQ = 16
PAST_LEN = 2048

CHUNK = 64
MIX_WIDTH = D_MODEL
S5_WIDTH = MIX_WIDTH // 2
S5_GROUP = 16
S5_GROUPS = S5_WIDTH // S5_GROUP
S5_STATE = 64
HG_WIDTH = MIX_WIDTH - S5_WIDTH
HG_HEADS = 16
HG_DK = 128
HG_DV = HG_WIDTH // HG_HEADS
HG_FDIM = HG_HEADS * HG_DK
IN_WIDTH = S5_WIDTH + 2 * HG_FDIM + 2 * HG_WIDTH
D_FF = -(-8 * D_MODEL // (3 * 256)) * 256
EPS = 1e-6
F_TINY = 1e-30

kernel_name = "hymba_s5_hgrn2_streaming_step"


def rms_norm(x, gain):
    xf = x.astype(jnp.float32)
    y = xf * lax.rsqrt(jnp.mean(xf * xf, axis=-1, keepdims=True) + EPS)
    return (y * gain.astype(jnp.float32)).astype(x.dtype)


def _complex_affine_combine(e1, e2):
    a1r, a1i, b1r, b1i = e1
    a2r, a2i, b2r, b2i = e2
    return (a2r * a1r - a2i * a1i,
            a2r * a1i + a2i * a1r,
            a2r * b1r - a2i * b1i + b2r,
            a2r * b1i + a2i * b1r + b2i)


def s5_group(u, h0_re, h0_im, a_re, a_im, log_dt, b_re, b_im, c_re, c_im, d, w_glu, out_gain):
    f32 = jnp.float32
    bsz, t, _ = u.shape
    uf = u.astype(f32).reshape(bsz, t, S5_GROUPS, S5_GROUP)
    a_re = a_re.astype(f32)
    a_im = a_im.astype(f32)
    dt = jnp.exp(log_dt.astype(f32))[:, None]
    mag = jnp.exp(a_re * dt)
    lam_re = mag * jnp.cos(a_im * dt)
    lam_im = mag * jnp.sin(a_im * dt)
    den = a_re * a_re + a_im * a_im
    coef_re = ((lam_re - 1.0) * a_re + lam_im * a_im) / den
    coef_im = (lam_im * a_re - (lam_re - 1.0) * a_im) / den
    b_re = b_re.astype(f32)
    b_im = b_im.astype(f32)
    bb_re = coef_re[..., None] * b_re - coef_im[..., None] * b_im
    bb_im = coef_re[..., None] * b_im + coef_im[..., None] * b_re
    bu_re = jnp.einsum('btgh,gph->btgp', uf, bb_re)
    bu_im = jnp.einsum('btgh,gph->btgp', uf, bb_im)
    h0_re = h0_re.astype(f32)
    h0_im = h0_im.astype(f32)
    bu_re = bu_re.at[:, 0].add(lam_re * h0_re - lam_im * h0_im)
    bu_im = bu_im.at[:, 0].add(lam_re * h0_im + lam_im * h0_re)
    lr = jnp.broadcast_to(lam_re, bu_re.shape)
    li = jnp.broadcast_to(lam_im, bu_im.shape)
    _, _, h_re, h_im = lax.associative_scan(_complex_affine_combine, (lr, li, bu_re, bu_im), axis=1)
    y = (jnp.einsum('btgp,ghp->btgh', h_re, c_re.astype(f32))
         - jnp.einsum('btgp,ghp->btgh', h_im, c_im.astype(f32))
         + d.astype(f32) * uf)
    y = y.reshape(bsz, t, S5_WIDTH)
    z = jax.nn.gelu(y, approximate=False)
    z = z * jax.nn.sigmoid(z @ w_glu.astype(f32))
    return rms_norm(z, out_gain).astype(u.dtype), h_re[:, -1], h_im[:, -1]


def hgrn2_group(q, fz, vi, g, s0, lb, out_gain):
    f32 = jnp.float32
    bsz, t, _ = q.shape
    qf = jax.nn.silu(q.astype(f32)).reshape(bsz, t, HG_HEADS, HG_DK)
    z = fz.astype(f32).reshape(bsz, t, HG_HEADS, HG_DK)
    lb = lb.astype(f32).reshape(HG_HEADS, HG_DK)
    f = lb + (1.0 - lb) * jax.nn.sigmoid(z)
    logf = jnp.log(jnp.maximum(f, F_TINY))
    kf = (1.0 - lb) * jax.nn.sigmoid(-z)
    vf = vi.astype(f32).reshape(bsz, t, HG_HEADS, HG_DV)
    c = min(CHUNK, t)
    n = t // c

    def to_chunks(a):
        return a.reshape(bsz, n, c, HG_HEADS, a.shape[-1]).transpose(1, 0, 3, 2, 4)

    mask = jnp.tril(jnp.ones((c, c), dtype=bool))[:, :, None]

    def step(s, inp):
        qc, lc, kc, vc = inp
        bc = jnp.cumsum(lc, axis=2)
        blast = bc[:, :, -1:, :]
        o_inter = jnp.einsum('bhtk,bhkv->bhtv', qc * jnp.exp(bc), s)
        diff = bc[:, :, :, None, :] - bc[:, :, None, :, :]
        decay = jnp.where(mask, jnp.exp(jnp.minimum(diff, 0.0)), 0.0)
        att = jnp.einsum('bhtk,bhtsk,bhsk->bhts', qc, decay, kc)
        o = o_inter + jnp.einsum('bhts,bhsv->bhtv', att, vc)
        s_new = (jnp.exp(blast[:, :, 0, :])[..., None] * s
                 + jnp.einsum('bhsk,bhsv->bhkv', kc * jnp.exp(blast - bc), vc))
        return s_new, o

    s_fin, o = lax.scan(step, s0.astype(f32),
                        (to_chunks(qf), to_chunks(logf), to_chunks(kf), to_chunks(vf)))
    o = o.transpose(1, 0, 3, 2, 4).reshape(bsz, t, HG_HEADS, HG_DV)
    gate = jax.nn.silu(g.astype(f32).reshape(bsz, t, HG_HEADS, HG_DV))
    out = (rms_norm(o, out_gain) * gate).reshape(bsz, t, HG_WIDTH)
    return out.astype(q.dtype), s_fin


def trunk_layer(x, h0_re, h0_im, s0, lb, w_in, w_out, norm_mix, norm_ffn,
                a_re, a_im, log_dt, b_re, b_im, c_re, c_im, d, w_glu, s5_gain,
                hg_gain, w_gate, w_up, w_down):
    h = rms_norm(x, norm_mix)
    proj = h @ w_in
    o1 = S5_WIDTH
    o2 = o1 + HG_FDIM
    o3 = o2 + HG_FDIM
    o4 = o3 + HG_WIDTH
    u, q, fz, vi, g = proj[..., :o1], proj[..., o1:o2], proj[..., o2:o3], proj[..., o3:o4], proj[..., o4:]
    s5_out, h_re, h_im = s5_group(u, h0_re, h0_im, a_re, a_im, log_dt, b_re, b_im,
                                  c_re, c_im, d, w_glu, s5_gain)
    hg_out, s_new = hgrn2_group(q, fz, vi, g, s0, lb, hg_gain)
    x = x + jnp.concatenate([s5_out, hg_out], axis=-1) @ w_out
    h = rms_norm(x, norm_ffn)
    x = x + (jax.nn.silu(h @ w_gate) * (h @ w_up)) @ w_down
    return x, h_re, h_im, s_new


def setup_inputs(seed: int = 0) -> dict:
    key = jax.random.key(seed)
    ks = jax.random.split(key, 24)
    f32 = jnp.float32
    nrm = lambda k, shape, scale: scale * jax.random.normal(k, shape, f32)
    return {
        'x_prompt': nrm(ks[0], (BATCH, SEQ, D_MODEL), 1.0),
        'x_sample': nrm(ks[1], (DEC_BATCH, DEC_SEQ, D_MODEL), 1.0),
        'state_s5_re': nrm(ks[2], (DEPTH, DEC_BATCH, S5_GROUPS, S5_STATE), 0.5),
        'state_s5_im': nrm(ks[3], (DEPTH, DEC_BATCH, S5_GROUPS, S5_STATE), 0.5),
        'state_hgrn': nrm(ks[4], (DEPTH, DEC_BATCH, HG_HEADS, HG_DK, HG_DV), 0.5),
        'w_in': nrm(ks[5], (DEPTH, D_MODEL, IN_WIDTH), D_MODEL ** -0.5),
        'w_out': nrm(ks[6], (DEPTH, MIX_WIDTH, D_MODEL), MIX_WIDTH ** -0.5),
        'norm_mix': 1.0 + nrm(ks[7], (DEPTH, D_MODEL), 0.02),
        'norm_ffn': 1.0 + nrm(ks[8], (DEPTH, D_MODEL), 0.02),
        'norm_final': 1.0 + nrm(ks[9], (D_MODEL,), 0.02),
        's5_a_re': -0.5 + nrm(ks[10], (DEPTH, S5_GROUPS, S5_STATE), 0.01),
        's5_a_im': jnp.pi * jnp.arange(S5_STATE, dtype=f32) + nrm(ks[11], (DEPTH, S5_GROUPS, S5_STATE), 0.01),
        's5_log_dt': jax.random.uniform(ks[12], (DEPTH, S5_GROUPS), f32, math.log(1e-3), math.log(1e-1)),
        's5_b_re': nrm(ks[13], (DEPTH, S5_GROUPS, S5_STATE, S5_GROUP), (2 * S5_GROUP) ** -0.5),
        's5_b_im': nrm(ks[14], (DEPTH, S5_GROUPS, S5_STATE, S5_GROUP), (2 * S5_GROUP) ** -0.5),
        's5_c_re': nrm(ks[15], (DEPTH, S5_GROUPS, S5_GROUP, S5_STATE), (2 * S5_STATE) ** -0.5),
        's5_c_im': nrm(ks[16], (DEPTH, S5_GROUPS, S5_GROUP, S5_STATE), (2 * S5_STATE) ** -0.5),
        's5_d': nrm(ks[17], (DEPTH, S5_GROUPS, S5_GROUP), 1.0),
        's5_w_glu': nrm(ks[18], (DEPTH, S5_WIDTH, S5_WIDTH), S5_WIDTH ** -0.5),
        's5_out_norm': 1.0 + nrm(ks[19], (DEPTH, S5_WIDTH), 0.02),
        'hg_lower_bounds': nrm(ks[20], (DEPTH, HG_FDIM), 0.5),
        'hg_out_norm': 1.0 + nrm(ks[21], (DEPTH, HG_DV), 0.02),
        'ffn_w_gate': nrm(ks[22], (DEPTH, D_MODEL, D_FF), D_MODEL ** -0.5),
        'ffn_w_up': nrm(jax.random.fold_in(ks[22], 1), (DEPTH, D_MODEL, D_FF), D_MODEL ** -0.5),
        'ffn_w_down': nrm(ks[23], (DEPTH, D_FF, D_MODEL), D_FF ** -0.5),
    }


def reference(x_prompt, x_sample, state_s5_re, state_s5_im, state_hgrn,
              w_in, w_out, norm_mix, norm_ffn, norm_final,
              s5_a_re, s5_a_im, s5_log_dt, s5_b_re, s5_b_im, s5_c_re, s5_c_im, s5_d,
              s5_w_glu, s5_out_norm, hg_lower_bounds, hg_out_norm,
              ffn_w_gate, ffn_w_up, ffn_w_down):
    f32 = jnp.float32
    sm = jax.nn.softmax(hg_lower_bounds.astype(f32), axis=0)
    lbs = jnp.cumsum(sm, axis=0) - sm[0]
    nb = x_prompt.shape[0]
    zero_s5 = jnp.zeros((nb, S5_GROUPS, S5_STATE), f32)
    zero_hg = jnp.zeros((nb, HG_HEADS, HG_DK, HG_DV), f32)
    yp, ys = x_prompt, x_sample
    p_re, p_im, p_hg, s_re, s_im, s_hg = [], [], [], [], [], []
    for l in range(DEPTH):
        lw = (lbs[l], w_in[l], w_out[l], norm_mix[l], norm_ffn[l],
              s5_a_re[l], s5_a_im[l], s5_log_dt[l], s5_b_re[l], s5_b_im[l],
              s5_c_re[l], s5_c_im[l], s5_d[l], s5_w_glu[l], s5_out_norm[l],
              hg_out_norm[l], ffn_w_gate[l], ffn_w_up[l], ffn_w_down[l])
        yp, hr, hi, hs = trunk_layer(yp, zero_s5, zero_s5, zero_hg, *lw)
        p_re.append(hr)
        p_im.append(hi)
        p_hg.append(hs)
        ys, hr, hi, hs = trunk_layer(ys, state_s5_re[l], state_s5_im[l], state_hgrn[l], *lw)
        s_re.append(hr)
        s_im.append(hi)
        s_hg.append(hs)
    y_prompt = rms_norm(yp, norm_final)
    y_sample = rms_norm(ys, norm_final)
    new_s5_re_prompt = jnp.stack(p_re)
    new_s5_im_prompt = jnp.stack(p_im)
    new_hgrn_prompt = jnp.stack(p_hg)
    new_s5_re_sample = jnp.stack(s_re)
    new_s5_im_sample = jnp.stack(s_im)
    new_hgrn_sample = jnp.stack(s_hg)
    return (y_prompt, y_sample, new_s5_re_prompt, new_s5_im_prompt, new_hgrn_prompt,
            new_s5_re_sample, new_s5_im_sample, new_hgrn_sample)
```

```python
import math
import bisect
import numpy as np
import concourse.bass as bass
import concourse.mybir as mybir
from concourse.bass_utils import run_bass_kernel_spmd

F32 = mybir.dt.float32
BF16 = mybir.dt.bfloat16
AF = mybir.ActivationFunctionType
ALU = mybir.AluOpType

D = 4096
KT_D = 32
S5W = 2048
HGW = 2048
NH = 16
INW = 10240
DFF = 11008
EPS = 1e-6
TINY = 1e-30
NCORE = 8
TS = 16
PW = 256
FFN_CH = [8, 7, 7, 7, 7, 7]
PI = math.pi


class Buf:
    def __init__(self, ap, space, lo, hi):
        self.ap, self.space, self.lo, self.hi = ap, space, lo, hi

    def reg(self):
        return (self.space, self.lo, self.hi)


class Tracker:
    def __init__(self):
        self.starts = {}
        self.iv = {}
        self.keys = {}

    def _split(self, sp, pos):
        st = self.starts.setdefault(sp, [])
        iv = self.iv.setdefault(sp, {})
        i = bisect.bisect_right(st, pos) - 1
        if i >= 0:
            s = st[i]
            e, w, r = iv[s]
            if s < pos < e:
                iv[s] = [pos, w, list(r)]
                iv[pos] = [e, w, list(r)]
                st.insert(i + 1, pos)

    def _cover(self, sp, lo, hi):
        st = self.starts.setdefault(sp, [])
        iv = self.iv.setdefault(sp, {})
        self._split(sp, lo)
        self._split(sp, hi)
        i = bisect.bisect_left(st, lo)
        cur = lo
        out = []
        while cur < hi:
            if i < len(st) and st[i] == cur:
                out.append(cur)
                cur = iv[cur][0]
                i += 1
            else:
                nxt = st[i] if i < len(st) and st[i] < hi else hi
                iv[cur] = [nxt, None, []]
                st.insert(i, cur)
                out.append(cur)
                cur = nxt
                i += 1
        return out

    def access(self, regs_r, regs_w, tick):
        deps = []
        for (sp, lo, hi) in regs_r:
            if sp == 'key':
                ent = self.keys.setdefault(lo, [None, []])
                if ent[0] is not None:
                    deps.append(ent[0])
                ent[1].append(tick)
                continue
            for s in self._cover(sp, lo, hi):
                ent = self.iv[sp][s]
                if ent[1] is not None:
                    deps.append(ent[1])
                ent[2].append(tick)
        for (sp, lo, hi) in regs_w:
            if sp == 'key':
                ent = self.keys.setdefault(lo, [None, []])
                if ent[0] is not None:
                    deps.append(ent[0])
                deps.extend(ent[1])
                self.keys[lo] = [tick, []]
                continue
            ss = self._cover(sp, lo, hi)
            for s in ss:
                ent = self.iv[sp][s]
                if ent[1] is not None:
                    deps.append(ent[1])
                deps.extend(ent[2])
            st = self.starts[sp]
            iv = self.iv[sp]
            for s in ss[1:]:
                del iv[s]
                st.pop(bisect.bisect_left(st, s))
            iv[lo] = [hi, tick, []]
        return deps


class Sched:
    ENG = ['pe', 'act', 'dve', 'pool', 'sp']

    def __init__(self, nc):
        self.nc = nc
        self.prog = {e: [] for e in self.ENG}
        self.cnt = {e: 0 for e in self.ENG}
        self.pending = {e: False for e in self.ENG}
        self.sem = {e: nc.alloc_semaphore("c_" + e) for e in self.ENG}
        self.ndma = {'sp': 20, 'pool': 8, 'act': 4}
        self.dsem = {q: [nc.alloc_semaphore("d_%s%d" % (q, i)) for i in range(n)] for q, n in self.ndma.items()}
        self.dcnt = {q: [0] * n for q, n in self.ndma.items()}
        self.dnext = {q: 0 for q in self.ndma}
        self.waited = {e: {} for e in self.ENG}
        self.trk = Tracker()
        self.nops = 0

    def _emit_waits(self, eng, deps):
        need = {}
        for d in deps:
            if d is None:
                continue
            kind, who, val = d
            if kind == 'x':
                continue
            if kind == 'e' and who == eng and eng == 'pe':
                continue
            key = (kind, who) if kind == 'e' else (kind, who[0], who[1])
            if val > need.get(key, 0):
                need[key] = val
        for key, val in need.items():
            if self.waited[eng].get(key, 0) >= val:
                continue
            self.waited[eng][key] = val
            sem = self.sem[key[1]] if key[0] == 'e' else self.dsem[key[1]][key[2]]
            self.prog[eng].append(lambda e, sem=sem, val=val: e.wait_ge(sem, val))

    def op(self, eng, name, args, kw=None, reads=(), writes=(), signal=True):
        self.nops += 1
        kw = kw or {}
        n = self.cnt[eng] + 1
        tick = ('e', eng, n)
        deps = self.trk.access([b.reg() for b in reads], [b.reg() for b in writes], tick)
        deps = [d for d in deps if d != tick]
        self._emit_waits(eng, deps)
        if signal:
            sem = self.sem[eng]
            self.prog[eng].append(lambda e, name=name, args=args, kw=kw, sem=sem: getattr(e, name)(*args, **kw).then_inc(sem, 1))
            self.cnt[eng] = n
            self.pending[eng] = False
        else:
            self.prog[eng].append(lambda e, name=name, args=args, kw=kw: getattr(e, name)(*args, **kw))
            self.pending[eng] = True
        return tick

    def dma(self, q, out_ap, in_ap, reads=(), writes=(), extra=None):
        self.nops += 1
        k = self.dnext[q]
        self.dnext[q] = (k + 1) % self.ndma[q]
        prev = self.dcnt[q][k]
        val = prev + 16
        self.dcnt[q][k] = val
        tick = ('d', (q, k), val)
        rr = [b.reg() if isinstance(b, Buf) else ('key', b, 0) for b in reads]
        ww = [b.reg() if isinstance(b, Buf) else ('key', b, 0) for b in writes]
        deps = [d for d in self.trk.access(rr, ww, tick) if d != tick]
        if prev > 0:
            deps.append(('d', (q, k), prev))
        self._emit_waits(q, deps)
        sem = self.dsem[q][k]
        kw = extra or {}
        self.prog[q].append(lambda e, o=out_ap, i=in_ap, sem=sem, kw=kw: e.dma_start(out=o, in_=i, **kw).then_inc(sem, 16))
        return tick

    def custom(self, eng, fn, deps, reads=(), writes=()):
        tickdeps = self.trk.access([('key', b, 0) for b in reads], [], ('x', 0, 0))
        self._emit_waits(eng, list(deps) + [d for d in tickdeps if d[0] != 'x'])
        self.prog[eng].append(fn)

    def finish(self):
        deps = []
        for e in self.ENG:
            assert not self.pending[e], "engine %s has unsignaled trailing ops" % e
            if self.cnt[e] > 0:
                deps.append(('e', e, self.cnt[e]))
        for q in self.ndma:
            for k in range(self.ndma[q]):
                if self.dcnt[q][k] > 0:
                    deps.append(('d', (q, k), self.dcnt[q][k]))
        self._emit_waits('sp', deps)

    def run(self, block):
        prog = self.prog

        @block.tensor
        def _(e):
            for c in prog['pe']:
                c(e)

        @block.scalar
        def _(e):
            for c in prog['act']:
                c(e)

        @block.vector
        def _(e):
            for c in prog['dve']:
                c(e)

        @block.gpsimd
        def _(e):
            for c in prog['pool']:
                c(e)

        @block.sync
        def _(e):
            for c in prog['sp']:
                c(e)


class Arena:
    def __init__(self, nc, name, nbytes, space='sb'):
        self.space = space
        self.nbytes = nbytes
        self.t = nc.alloc_sbuf_tensor(name, [128, nbytes // 4], F32)
        self.base = self.t.ap() if hasattr(self.t, 'ap') else self.t[:]
        self.top = 0
        self.marks = []

    def alloc(self, nbytes, dtype=F32, shape=None):
        exact = nbytes
        nbytes = (nbytes + 31) // 32 * 32
        lo = self.top
        self.top += nbytes
        assert self.top <= self.nbytes, "arena overflow %d > %d" % (self.top, self.nbytes)
        ap = self.base[:, lo // 4: (lo + nbytes) // 4]
        esz = 4
        if dtype != F32:
            ap = ap.bitcast(dtype)
            esz = 2
        ap = ap[:, :exact // esz]
        return Buf(ap, self.space, lo, lo + nbytes)

    def mark(self):
        self.marks.append(self.top)

    def release(self):
        self.top = self.marks.pop()


def sub(buf, e0, n, esize):
    return Buf(buf.ap[:, e0:e0 + n], buf.space, buf.lo + e0 * esize, buf.lo + (e0 + n) * esize)


def lay_panels(w, kt0, ktn):
    K, M = w.shape
    wk = w[kt0 * 128:(kt0 + ktn) * 128]
    a = wk.reshape(ktn, 128, M // PW, PW).transpose(2, 1, 0, 3)
    return np.ascontiguousarray(a).reshape(M // PW * 128, ktn * PW)


def col_tiles(v, nt):
    v2 = v.reshape(-1, nt, 128)
    return np.ascontiguousarray(v2.transpose(2, 0, 1)).reshape(128, -1)


def s5_col(a):
    L = a.shape[0]
    b = a.reshape(L, 64, 2, 64).transpose(2, 3, 0, 1)
    return np.ascontiguousarray(b).reshape(128, L * 64)


def s5_row_idx():
    part = np.arange(128)[:, None, None]
    slot = np.arange(32)[None, :, None]
    col = np.arange(128)[None, None, :]
    a = part // 64
    j = 4 * (slot // 2) + 2 * a + slot % 2
    g = 2 * j + col // 64
    p = col % 64 + 0 * g
    return g, p


def build_host_inputs(inp, depth, TP):
    N = TP + TS
    f = np.float32
    sh = {}
    w_in = np.asarray(inp['w_in'], f)
    sh['w_in'] = np.concatenate([lay_panels(w_in[l], 0, 32) for l in range(depth)], 0)
    w_out = np.asarray(inp['w_out'], f)
    sh['w_out'] = np.concatenate([lay_panels(w_out[l], 0, 32) for l in range(depth)], 0)
    w_glu = np.asarray(inp['s5_w_glu'], f)
    sh['w_glu'] = np.concatenate([lay_panels(w_glu[l], 0, 16) for l in range(depth)], 0)
    wg = np.asarray(inp['ffn_w_gate'], f)
    sh['w_gate'] = np.concatenate([lay_panels(wg[l], 0, 32) for l in range(depth)], 0)
    wu = np.asarray(inp['ffn_w_up'], f)
    sh['w_up'] = np.concatenate([lay_panels(wu[l], 0, 32) for l in range(depth)], 0)
    wd = np.asarray(inp['ffn_w_down'], f)
    p0 = 0
    for ci, npan in enumerate(FFN_CH):
        sh['w_down%d' % ci] = np.concatenate([lay_panels(wd[l], 2 * p0, 2 * npan) for l in range(depth)], 0)
        p0 += npan
    sh['g_mix'] = col_tiles(np.asarray(inp['norm_mix'], f), 32)
    sh['g_ffn'] = col_tiles(np.asarray(inp['norm_ffn'], f), 32)
    sh['g_fin'] = col_tiles(np.asarray(inp['norm_final'], f)[None], 32)
    sh['g_s5'] = col_tiles(np.asarray(inp['s5_out_norm'], f), 16)
    sh['g_hg'] = col_tiles(np.asarray(inp['hg_out_norm'], f), 1)
    sh['lb_raw'] = col_tiles(np.asarray(inp['hg_lower_bounds'], f), 16)
    a_re = np.asarray(inp['s5_a_re'], f)
    a_im = np.asarray(inp['s5_a_im'], f)
    ldt = np.asarray(inp['s5_log_dt'], f)
    ldt_full = np.broadcast_to(ldt[:, :, None], a_re.shape)
    sh['are_c'] = s5_col(a_re)
    sh['aim_c'] = s5_col(a_im)
    sh['ldt_c'] = s5_col(np.ascontiguousarray(ldt_full))
    g, p = s5_row_idx()
    sh['are_r'] = np.concatenate([a_re[l][g, p].reshape(128, 4096) for l in range(depth)], 1)
    sh['aim_r'] = np.concatenate([a_im[l][g, p].reshape(128, 4096) for l in range(depth)], 1)
    sh['ldt_r'] = np.concatenate([ldt_full[l][g, p].reshape(128, 4096) for l in range(depth)], 1)
    b_re = np.asarray(inp['s5_b_re'], f)
    b_im = np.asarray(inp['s5_b_im'], f)
    c_re = np.asarray(inp['s5_c_re'], f)
    c_im = np.asarray(inp['s5_c_im'], f)

    def lbpad(b):
        out = np.zeros((128, 32, 128), f)
        for j in range(64):
            J, q = j // 4, j % 4
            a, slot = q // 2, 2 * J + q % 2
            for two in range(2):
                r0 = 64 * a + (q % 2) * 32 + two * 16
                out[r0:r0 + 16, slot, two * 64:(two + 1) * 64] = b[2 * j + two].T
        return out.reshape(128, 4096)

    def lcpad(c):
        out = np.zeros((128, 64, 64), f)
        for j in range(64):
            q = j % 4
            for two in range(2):
                c0 = (q % 2) * 32 + two * 16
                out[two * 64:(two + 1) * 64, j, c0:c0 + 16] = c[2 * j + two].T
        return out.reshape(128, 4096)

    sh['lb_re'] = np.concatenate([lbpad(b_re[l]) for l in range(depth)], 1)
    sh['lb_im'] = np.concatenate([lbpad(b_im[l]) for l in range(depth)], 1)
    sh['lc_re'] = np.concatenate([lcpad(c_re[l]) for l in range(depth)], 1)
    sh['lc_im'] = np.concatenate([lcpad(c_im[l]) for l in range(depth)], 1)
    sh['d_c'] = col_tiles(np.asarray(inp['s5_d'], f).reshape(depth, 2048), 16)
    B = min(256, TP)
    cst = np.zeros((128, 1024 + N), f)
    cst[:, 0:B + 1] = np.arange(B + 1, dtype=f)[None]
    cst[:, 300:300 + 128] = np.eye(128, dtype=f)
    s_ = np.arange(64)[:, None]
    t_ = np.arange(64)[None, :]
    cst[:64, 430:494] = (s_ <= t_).astype(f)
    cst[:, 512:768] = 1.0
    m01 = np.ones(N, f)
    m01[0:TP:64] = 0.0
    m01[TP] = 0.0
    cst[:, 1024:1024 + N] = m01[None]
    sh['cst'] = cst
    xp = np.asarray(inp['x_prompt'], f)
    xs = np.asarray(inp['x_sample'], f)
    s_re = np.asarray(inp['state_s5_re'], f)
    s_im = np.asarray(inp['state_s5_im'], f)
    s_hg = np.asarray(inp['state_hgrn'], f)
    per = []
    for c in range(NCORE):
        d = {}
        xc = np.concatenate([xp[0, c * TP:(c + 1) * TP], xs[c]], 0)
        d['xT'] = np.ascontiguousarray(xc.T)
        d['h0_re'] = s5_col(s_re[:depth, c])
        d['h0_im'] = s5_col(s_im[:depth, c])
        d['s0_hg'] = np.ascontiguousarray(s_hg[:depth, c].transpose(2, 0, 1, 3)).reshape(128, depth * NH * 128)
        cm = np.zeros((128, 8), f)
        cm[:, :c] = 1.0
        d['cmask'] = cm
        per.append(d)
    return sh, per


def build_program(depth, TP, shapes, kstop=99):
    N = TP + TS
    B = min(256, TP)
    NBLK = TP // B
    NCHK = TP // 64
    nc = bass.Bass("TRN2", target_bir_lowering=False)
    dr = {}
    for name, shp in shapes.items():
        dr[name] = nc.dram_tensor(name, list(shp), F32, kind="ExternalInput").ap()
    yT = nc.dram_tensor("yT", [D, N], F32, kind="ExternalOutput").ap()
    o_s5p = nc.dram_tensor("o_s5p", [128, depth * 128], F32, kind="ExternalOutput").ap()
    o_s5s = nc.dram_tensor("o_s5s", [128, depth * 128], F32, kind="ExternalOutput").ap()
    o_hgp = nc.dram_tensor("o_hgp", [128, depth * NH * 128], F32, kind="ExternalOutput").ap()
    o_hgs = nc.dram_tensor("o_hgs", [128, depth * NH * 128], F32, kind="ExternalOutput").ap()
    X = nc.dram_tensor("X", [D, N], F32).ap()
    PROJ = nc.dram_tensor("PROJ", [INW, N], F32).ap()
    YLOC = nc.dram_tensor("YLOC", [S5W, N], F32).ap()
    OLOC = nc.dram_tensor("OLOC", [HGW, N], F32).ap()
    QHAT = nc.dram_tensor("QHAT", [HGW, TP], BF16).ap()
    GW = NH * 128 + NH + 128
    GINt = [nc.dram_tensor("GIN%d" % l, [128, GW], F32) for l in range(depth)]
    GOUTt = [nc.dram_tensor("GOUT%d" % l, [NCORE * 128, GW], F32) for l in range(depth)]
    GIN = [t.ap() for t in GINt]
    GOUT = [t.ap() for t in GOUTt]

    ccs = [nc.alloc_semaphore("cc%d" % l) for l in range(depth)]
    S = Sched(nc)
    AR = Arena(nc, "arena", 206 * 1024)
    pst = nc.alloc_psum_tensor("psA", [128, 7 * 512], F32)
    psA = pst.ap()
    pst2 = nc.alloc_psum_tensor("psT", [128, 1024], BF16)
    psT = pst2.ap()

    def bank(i, n=512, off=0):
        nb_ = ((off + n) * 4 + 2047) // 2048
        return Buf(psA[:, i * 512 + off: i * 512 + off + n], 'ps', i * 2048, (i + nb_) * 2048)

    def bankT(off, n):
        return Buf(psT[:, off:off + n], 'ps', 7 * 2048, 8 * 2048)


    TB = []
    t0 = 0
    while t0 < N:
        sz = min(512, N - t0)
        TB.append((t0, sz))
        t0 += sz
    NTB = len(TB)
    assert 2 * NTB <= 6
    STB = [6, 5, 4]

    actT = AR.alloc(32 * N * 2, BF16)
    cst = AR.alloc((1024 + N) * 4)
    cbf = AR.alloc(512 * 2, BF16)
    gains = AR.alloc((depth * 64 + 32 + depth * 16 + depth) * 4)
    lbt = AR.alloc(depth * 16 * 4 * 4)
    s5c = AR.alloc(depth * 64 * 3 * 4)
    dcol = AR.alloc(depth * 16 * 4)
    h0c = AR.alloc(depth * 64 * 2 * 4)
    cmask = AR.alloc(8 * 4)
    rstd = AR.alloc(N * 4)
    ostage = AR.alloc(256 * 4)
    kce = AR.alloc(128 * 4)
    kcin = AR.alloc(128 * 4)
    c1s1 = AR.alloc(128 * 4)
    lamT = AR.alloc(128 * 4)
    Dloc = AR.alloc(NH * 4)
    cb = AR.alloc(8 * 4)
    ft = [AR.alloc(N * 4) for _ in range(6)]
    bt = [AR.alloc(N * 2, BF16) for _ in range(4)]
    P_OVER = AR.top
    wb = [AR.alloc(32 * PW * 2, BF16) for _ in range(2)]
    act2 = AR.alloc(16 * N * 2, BF16)
    KC = 8
    wstage = [AR.alloc(KC * PW * 4) for _ in range(2)]
    P_END = AR.top

    IOTA = lambda n: sub(cst, 0, n, 4)
    IOTA1 = lambda n: sub(cst, 1, n, 4)
    IDF = sub(cst, 300, 128, 4)
    MSKF = sub(cst, 430, 64, 4)
    ONES = lambda n: sub(cst, 512, n, 4)
    M01 = sub(cst, 1024, N, 4)
    IDB = sub(cbf, 0, 128, 2)
    MSKB = sub(cbf, 128, 64, 2)
    ONEB = sub(cbf, 192, 128, 2)
    NEGPI = sub(cb, 0, 1, 4)

    def dve(name, *args, r=(), w=(), **kw):
        return S.op('dve', name, args, kw, r, w)

    def act(name, *args, r=(), w=(), **kw):
        return S.op('act', name, args, kw, r, w)

    def pe(name, *args, r=(), w=(), signal=True, **kw):
        return S.op('pe', name, args, kw, r, w, signal=signal)

    def tt(out, a, b, op):
        return dve('tensor_tensor', out.ap, a.ap, b.ap, op, r=[a, b], w=[out])

    def ttv(out, oap, a, aap, b, bap, op):
        return dve('tensor_tensor', oap, aap, bap, op, r=[a, b], w=[out])

    def ts(out, a, s1, s2, op0, op1=None, extra_r=()):
        if op1 is None:
            return dve('tensor_scalar', out.ap, a.ap, s1, None, op0, r=[a] + list(extra_r), w=[out])
        return dve('tensor_scalar', out.ap, a.ap, s1, s2, op0, op1, r=[a] + list(extra_r), w=[out])

    def stt(out, a, sc, b, op0, op1, extra_r=()):
        return dve('scalar_tensor_tensor', out.ap, a.ap, sc, b.ap, op0, op1, r=[a, b] + list(extra_r), w=[out])

    def actf(out, a, func, extra_r=(), **kw):
        return act('activation', out.ap, a.ap, func, r=[a] + list(extra_r), w=[out], **kw)

    def cp(out, a):
        return dve('tensor_copy', out.ap, a.ap, r=[a], w=[out])

    def ms(out, val):
        return dve('memset', out.ap, val, r=[], w=[out])

    def load(dst, src_ap, key=None, q='sp'):
        return S.dma(q, dst.ap, src_ap, reads=[key] if key else [], writes=[dst])

    def store(dst_ap, src, key=None, q='sp'):
        return S.dma(q, dst_ap, src.ap, reads=[src], writes=[key] if key else [])

    load(cst, dr['cst'])
    o = 0
    g_mix = sub(gains, o, depth * 32, 4); o += depth * 32
    g_ffn = sub(gains, o, depth * 32, 4); o += depth * 32
    g_fin = sub(gains, o, 32, 4); o += 32
    g_s5 = sub(gains, o, depth * 16, 4); o += depth * 16
    g_hg = sub(gains, o, depth, 4); o += depth
    load(g_mix, dr['g_mix']); load(g_ffn, dr['g_ffn']); load(g_fin, dr['g_fin'])
    load(g_s5, dr['g_s5']); load(g_hg, dr['g_hg'])
    are_c = sub(s5c, 0, depth * 64, 4)
    aim_c = sub(s5c, depth * 64, depth * 64, 4)
    ldt_c = sub(s5c, 2 * depth * 64, depth * 64, 4)
    load(are_c, dr['are_c']); load(aim_c, dr['aim_c']); load(ldt_c, dr['ldt_c'])
    load(dcol, dr['d_c'])
    h0re = sub(h0c, 0, depth * 64, 4)
    h0im = sub(h0c, depth * 64, depth * 64, 4)
    load(h0re, dr['h0_re']); load(h0im, dr['h0_im'])
    load(cmask, dr['cmask'])
    cp(IDB, IDF)
    dve('tensor_copy', MSKB.ap[:64], MSKF.ap[:64], r=[MSKF], w=[MSKB])
    ms(ONEB, 1.0)
    ms(NEGPI, -PI)

    lbs = sub(lbt, 0, depth * 16, 4)
    oml = sub(lbt, depth * 16, depth * 16, 4)
    noml = sub(lbt, 2 * depth * 16, depth * 16, 4)
    ltmp = sub(lbt, 3 * depth * 16, depth * 16, 4)
    AR.mark()
    AR.top = P_END
    lraw = AR.alloc(depth * 16 * 4)
    lmx = AR.alloc(16 * 4)
    lsum = AR.alloc(16 * 4)
    load(lraw, dr['lb_raw'])
    L_ = lambda b, l: sub(b, l * 16, 16, 4)
    cp(lmx, L_(lraw, 0))
    for l in range(1, depth):
        tt(lmx, lmx, L_(lraw, l), ALU.max)
    for l in range(depth):
        tt(L_(ltmp, l), L_(lraw, l), lmx, ALU.subtract)
    actf(ltmp, ltmp, AF.Exp)
    cp(lsum, L_(ltmp, 0))
    for l in range(1, depth):
        tt(lsum, lsum, L_(ltmp, l), ALU.add)
    dve('reciprocal', lsum.ap, lsum.ap, r=[lsum], w=[lsum])
    for l in range(depth):
        tt(L_(ltmp, l), L_(ltmp, l), lsum, ALU.mult)
    ms(L_(lbs, 0), 0.0)
    for l in range(1, depth):
        tt(L_(lbs, l), L_(lbs, l - 1), L_(ltmp, l), ALU.add)
    ts(oml, lbs, -1.0, 1.0, ALU.mult, ALU.add)
    ts(noml, oml, -1.0, None, ALU.mult)
    AR.release()

    for t in range(32):
        b_ = ft[t % 2]
        load(b_, dr['xT'][t * 128:(t + 1) * 128, :])
        store(X[t * 128:(t + 1) * 128, :], b_, key=('X', t))

    def stats_accum(srcs, scale):
        nt = len(srcs)
        for i, xt_ in enumerate(srcs):
            sq = bt[i % 2]
            actf(sq, xt_, AF.Square)
            for bi, (t0, sz) in enumerate(TB):
                pb = bank(STB[bi], sz)
                pe('matmul', pb.ap, ONEB.ap, sq.ap[:, t0:t0 + sz], start=(i == 0), stop=(i == nt - 1),
                   r=[ONEB, sq], w=[pb], signal=(bi == NTB - 1 or i == nt - 1))
        stats_finish(scale)

    def stats_finish(scale):
        for bi, (t0, sz) in enumerate(TB):
            pb = bank(STB[bi], sz)
            r_ = sub(rstd, t0, sz, 4)
            ts(r_, pb, scale, EPS, ALU.mult, ALU.add)
        actf(rstd, rstd, AF.Ln)
        actf(rstd, rstd, AF.Exp, scale=-0.5)

    def stats_stream(nt, loader, scale):
        for i in range(nt):
            xt_ = loader(i)
            sq = bt[i % 2]
            actf(sq, xt_, AF.Square)
            for bi, (t0, sz) in enumerate(TB):
                pb = bank(STB[bi], sz)
                pe('matmul', pb.ap, ONEB.ap, sq.ap[:, t0:t0 + sz], start=(i == 0), stop=(i == nt - 1),
                   r=[ONEB, sq], w=[pb], signal=(bi == NTB - 1 or i == nt - 1))
        stats_finish(scale)

    def ldX(i, slot0=0):
        b_ = ft[slot0 + i % 2]
        load(b_, X[i * 128:(i + 1) * 128, :], key=('X', i))
        return b_

    def x_norm_to_act(gbuf, goff):
        stats_stream(32, lambda i: ldX(i, 0), 1.0 / D)
        for i in range(32):
            b_ = ldX(i, 2)
            o_ = sub(actT, i * N, N, 2)
            g_ = sub(gbuf, goff + i, 1, 4)
            stt(o_, b_, g_.ap, rstd, ALU.mult, ALU.mult, extra_r=[g_])

    wslot = [0, 0]
    tile_ctr = [0]

    def dense(wname, row0, ktn, npanels, rhs, evac):
        wd = dr[wname]
        for p in range(npanels):
            slot = wslot[0] % 2
            wslot[0] += 1
            wbuf = sub(wb[slot], 0, ktn * PW, 2)
            for c0_ in range(0, ktn, KC):
                kc_ = min(KC, ktn - c0_)
                sv = sub(wstage[wslot[1] % 2], 0, kc_ * PW, 4)
                wslot[1] += 1
                load(sv, wd[row0 + p * 128: row0 + (p + 1) * 128, c0_ * PW:(c0_ + kc_) * PW])
                wv = sub(wbuf, c0_ * PW, kc_ * PW, 2)
                S.op('pool', 'tensor_copy', (wv.ap, sv.ap), {}, [sv], [wv])
            for mt in range(2):
                st = tile_ctr[0] % 2
                tile_ctr[0] += 1
                banks = [bank(st * NTB + bi, sz) for bi, (t0, sz) in enumerate(TB)]
                for k in range(ktn):
                    rk_ = sub(rhs, k * N, N, 2)
                    for bi, (t0, sz) in enumerate(TB):
                        pe('matmul', banks[bi].ap, wbuf.ap[:, k * PW + mt * 128: k * PW + (mt + 1) * 128],
                           rhs.ap[:, k * N + t0: k * N + t0 + sz], start=(k == 0), stop=(k == ktn - 1),
                           r=[sub(wbuf, k * PW, PW, 2), rk_], w=[banks[bi]], signal=(k == ktn - 1))
                evac(p * 2 + mt, banks)

    ev_ctr = [0]

    def evac_to_dram(dst, key):
        def f(mi, banks):
            b_ = ft[4 + ev_ctr[0] % 2]
            ev_ctr[0] += 1
            for bi, (t0, sz) in enumerate(TB):
                o_ = sub(b_, t0, sz, 4)
                if (mi + bi) % 2 == 0:
                    actf(o_, banks[bi], AF.Copy)
                else:
                    cp(o_, banks[bi])
            store(dst[mi * 128:(mi + 1) * 128, :], b_, key=(key, mi))
        return f

    def evac_resid(mi, banks):
        b_ = ft[4 + ev_ctr[0] % 2]
        ev_ctr[0] += 1
        load(b_, X[mi * 128:(mi + 1) * 128, :], key=('X', mi))
        for bi, (t0, sz) in enumerate(TB):
            o_ = sub(b_, t0, sz, 4)
            tt(o_, o_, banks[bi], ALU.add)
        store(X[mi * 128:(mi + 1) * 128, :], b_, key=('X', mi))

    def sincos(out_c, out_s, ang, tmp):
        MAG = 12582912.0
        PIC = 3.1415925
        ts(tmp, ang, 1.0 / (2 * PI), MAG, ALU.mult, ALU.add)
        ts(tmp, tmp, -MAG, -2 * PI, ALU.add, ALU.mult)
        tt(tmp, ang, tmp, ALU.add)
        ts(tmp, tmp, -PIC, PIC, ALU.max, ALU.min)
        actf(out_s, tmp, AF.Sin)
        ts(out_c, ang, 0.5 * PI, None, ALU.add)
        ts(tmp, out_c, 1.0 / (2 * PI), MAG, ALU.mult, ALU.add)
        ts(tmp, tmp, -MAG, -2 * PI, ALU.add, ALU.mult)
        tt(tmp, out_c, tmp, ALU.add)
        ts(tmp, tmp, -PIC, PIC, ALU.max, ALU.min)
        actf(out_c, tmp, AF.Sin)

    def cmul(o_re, o_im, a_re, a_im, b_re, b_im, t1, t2):
        tt(t1, a_re, b_re, ALU.mult)
        tt(t2, a_im, b_im, ALU.mult)
        tt(o_re, t1, t2, ALU.subtract)
        tt(t1, a_re, b_im, ALU.mult)
        tt(t2, a_im, b_re, ALU.mult)
        tt(o_im, t1, t2, ALU.add)

    def v3(buf, Bn, stride=None):
        stride = stride or Bn
        return buf.ap[:, :4 * stride].rearrange("p (a b) -> p a b", a=4)[:, :, :Bn]

    c1 = sub(c1s1, 0, 64, 4); s1 = sub(c1s1, 64, 64, 4)
    lre = sub(lamT, 0, 64, 4); lim = sub(lamT, 64, 64, 4)
    kir = sub(kcin, 0, 64, 4); kii = sub(kcin, 64, 64, 4)

    for l in range(depth):
        x_norm_to_act(g_mix, l * 32)
        dense('w_in', l * 40 * 128, 32, 40, actT, evac_to_dram(PROJ, 'PROJ'))

        AR.mark()
        AR.top = P_OVER
        p_dt = AR.alloc(64 * 4); p_adt = AR.alloc(64 * 4); p_r = AR.alloc(64 * 4); p_th = AR.alloc(64 * 4)
        p_t1 = AR.alloc(64 * 4); p_t2 = AR.alloc(64 * 4); p_t3 = AR.alloc(64 * 4); p_t4 = AR.alloc(64 * 4)
        arec = sub(are_c, l * 64, 64, 4); aimc = sub(aim_c, l * 64, 64, 4); ldtc = sub(ldt_c, l * 64, 64, 4)
        actf(p_dt, ldtc, AF.Exp)
        tt(p_adt, arec, p_dt, ALU.mult)
        actf(p_r, p_adt, AF.Exp)
        tt(p_th, aimc, p_dt, ALU.mult)
        sincos(c1, s1, p_th, p_t1)
        ts(p_t2, p_th, float(B), None, ALU.mult)
        sincos(p_t3, p_t4, p_t2, p_t1)
        actf(p_t1, p_adt, AF.Exp, scale=float(B))
        tt(lre, p_t3, p_t1, ALU.mult)
        tt(lim, p_t4, p_t1, ALU.mult)
        nb = NBLK
        while nb > 1:
            assert nb % 2 == 0
            cmul(p_t3, p_t4, lre, lim, lre, lim, p_t1, p_t2)
            cp(lre, p_t3)
            cp(lim, p_t4)
            nb //= 2

        def s5_tables(J, Bn, Er, Ei, R, tmp, mode):
            W = 4 * (Bn + 1)
            ang = sub(tmp, 0, W, 4)
            for jj in range(4):
                j = 4 * J + jj
                a_ = sub(tmp, jj * (Bn + 1), Bn + 1, 4)
                th_ = sub(p_th, j, 1, 4)
                ts(a_, IOTA(Bn + 1), th_.ap, None, ALU.mult, extra_r=[th_])
            sincos(sub(Er, 0, W, 4), sub(Ei, 0, W, 4), ang, sub(tmp, W, W, 4))
            for jj in range(4):
                j = 4 * J + jj
                r_ = sub(R, jj * Bn, Bn, 4)
                if mode == 'r0':
                    rc = sub(p_r, j, 1, 4)
                    ts(r_, ONES(Bn), rc.ap, None, ALU.mult, extra_r=[rc])
                    ms(sub(R, jj * Bn, 1, 4), 0.0)
                else:
                    ac = sub(p_adt, j, 1, 4)
                    actf(r_, IOTA1(Bn), AF.Exp, extra_r=[ac], scale=ac.ap)

        def rot_out(kre, kim, Er, Ei, Bn, hre, nhim, t1, t2):
            n = 4 * Bn
            k3r = v3(kre, Bn); k3i = v3(kim, Bn)
            e3r = v3(Er, Bn, Bn + 1); e3i = v3(Ei, Bn, Bn + 1)
            T1 = sub(t1, 0, n, 4); T2 = sub(t2, 0, n, 4)
            KR = sub(kre, 0, n, 4); KI = sub(kim, 0, n, 4)
            ttv(T1, v3(t1, Bn), KR, k3r, Er, e3r, ALU.mult)
            ttv(T2, v3(t2, Bn), KI, k3i, Ei, e3i, ALU.mult)
            tt(sub(hre, 0, n, 2), T1, T2, ALU.subtract)
            ttv(T1, v3(t1, Bn), KR, k3r, Ei, e3i, ALU.mult)
            ttv(T2, v3(t2, Bn), KI, k3i, Er, e3r, ALU.mult)
            stt(sub(nhim, 0, n, 2), T1, -1.0, T2, ALU.mult, ALU.subtract)

        def c_matmul(LCre, LCim, hre, nhim, Bn, py):
            for a in range(2):
                first = True
                for jj in (2 * a, 2 * a + 1):
                    for (LCx, hh) in ((LCre, hre), (LCim, nhim)):
                        last = (jj == 2 * a + 1) and (LCx is LCim)
                        pe('matmul', py.ap[64 * a:64 * a + 64, :Bn], LCx.ap[:, jj * 64:(jj + 1) * 64],
                           hh.ap[:, jj * Bn:(jj + 1) * Bn], start=first, stop=last,
                           r=[LCx, sub(hh, 0, 4 * Bn, 2)], w=[py], signal=last)
                        first = False

        def carry_rot(kc_re, kc_im, kre, kim, Er, Ei, Bn, t1, t2):
            n = 4 * Bn
            KR = sub(kre, 0, n, 4); KI = sub(kim, 0, n, 4)
            klr = v3(kre, Bn)[:, :, Bn - 1]; kli = v3(kim, Bn)[:, :, Bn - 1]
            cE = v3(Er, Bn + 1, Bn + 1)[:, :, Bn]; sE = v3(Ei, Bn + 1, Bn + 1)[:, :, Bn]
            T1 = sub(t1, 0, 4, 4); T2 = sub(t2, 0, 4, 4)
            dve('tensor_tensor', T1.ap, klr, cE, ALU.mult, r=[KR, Er], w=[T1])
            dve('tensor_tensor', T2.ap, kli, sE, ALU.mult, r=[KI, Ei], w=[T2])
            tt(kc_re, T1, T2, ALU.subtract)
            dve('tensor_tensor', T1.ap, klr, sE, ALU.mult, r=[KR, Ei], w=[T1])
            dve('tensor_tensor', T2.ap, kli, cE, ALU.mult, r=[KI, Er], w=[T2])
            tt(kc_im, T1, T2, ALU.add)

        if kstop < 2:
            AR.marks = []
            break
        AR.mark()
        W1 = 4 * (B + 1)
        Er = AR.alloc(W1 * 4); Ei = AR.alloc(W1 * 4); R0 = AR.alloc(4 * B * 4)
        Ers = AR.alloc(4 * 17 * 4); Eis = AR.alloc(4 * 17 * 4); R0s = AR.alloc(64 * 4)
        tmpA = AR.alloc(2 * W1 * 4)
        t1 = AR.alloc(4 * B * 4); t2 = AR.alloc(4 * B * 4)
        btr = AR.alloc(4 * B * 4); bti = AR.alloc(4 * B * 4)
        kre = AR.alloc(4 * B * 4); kim = AR.alloc(4 * B * 4)
        hre = AR.alloc(4 * B * 2, BF16); nhim = AR.alloc(4 * B * 2, BF16)
        prow = AR.alloc(256 * 4 * 10)
        LBp = AR.alloc(256 * 4 * 2)
        LB = AR.alloc(256 * 2 * 2, BF16)
        LC = AR.alloc(256 * 2 * 2, BF16)
        LCf = AR.alloc(512 * 4)
        ubf = AR.alloc(N * 2, BF16)
        kc = AR.alloc(8 * 4); rk = AR.alloc(8 * 4)
        PR = lambda i: sub(prow, i * 256, 256, 4)
        kcr = sub(kc, 0, 4, 4); kci = sub(kc, 4, 4, 4)
        rkr = sub(rk, 0, 4, 4); rki = sub(rk, 4, 4, 4)
        Bre = sub(LBp, 0, 256, 4); Bim = sub(LBp, 256, 256, 4)
        LBre = sub(LB, 0, 256, 2); LBim = sub(LB, 256, 256, 2)
        LCre = sub(LC, 0, 256, 2); LCim = sub(LC, 256, 256, 2)
        for J in range(16):
            c0 = l * 4096 + J * 256
            load(PR(0), dr['are_r'][:, c0:c0 + 256]); load(PR(1), dr['aim_r'][:, c0:c0 + 256])
            load(PR(2), dr['ldt_r'][:, c0:c0 + 256])
            load(Bre, dr['lb_re'][:, c0:c0 + 256]); load(Bim, dr['lb_im'][:, c0:c0 + 256])
            load(sub(LCf, 0, 256, 4), dr['lc_re'][:, c0:c0 + 256]); load(sub(LCf, 256, 256, 4), dr['lc_im'][:, c0:c0 + 256])
            S.op('pool', 'tensor_copy', (LC.ap, LCf.ap), {}, [LCf], [LC])
            actf(PR(3), PR(2), AF.Exp)
            tt(PR(4), PR(0), PR(3), ALU.mult)
            actf(PR(4), PR(4), AF.Exp)
            tt(PR(5), PR(1), PR(3), ALU.mult)
            sincos(PR(6), PR(7), PR(5), PR(8))
            tt(PR(6), PR(6), PR(4), ALU.mult)
            tt(PR(7), PR(7), PR(4), ALU.mult)
            ts(PR(6), PR(6), -1.0, None, ALU.add)
            tt(PR(3), PR(0), PR(0), ALU.mult)
            tt(PR(4), PR(1), PR(1), ALU.mult)
            tt(PR(3), PR(3), PR(4), ALU.add)
            dve('reciprocal', PR(3).ap, PR(3).ap, r=[PR(3)], w=[PR(3)])
            tt(PR(4), PR(6), PR(0), ALU.mult); tt(PR(5), PR(7), PR(1), ALU.mult)
            tt(PR(4), PR(4), PR(5), ALU.add); tt(PR(8), PR(4), PR(3), ALU.mult)
            tt(PR(4), PR(7), PR(0), ALU.mult); tt(PR(5), PR(6), PR(1), ALU.mult)
            tt(PR(4), PR(4), PR(5), ALU.subtract); tt(PR(9), PR(4), PR(3), ALU.mult)
            tt(PR(4), Bre, PR(8), ALU.mult); tt(PR(5), Bim, PR(9), ALU.mult)
            tt(LBre, PR(4), PR(5), ALU.subtract)
            tt(PR(4), Bim, PR(8), ALU.mult); tt(PR(5), Bre, PR(9), ALU.mult)
            tt(LBim, PR(4), PR(5), ALU.add)
            s5_tables(J, B, Er, Ei, R0, tmpA, 'r0')
            s5_tables(J, 16, Ers, Eis, R0s, tmpA, 'r0')
            uf = ft[J % 2]
            load(uf, PROJ[J * 128:(J + 1) * 128, :], key=('PROJ', J))
            actf(ubf, uf, AF.Copy)
            yt = ft[2 + J % 2]
            dc = sub(dcol, l * 16 + J, 1, 4)
            rcol = sub(p_r, 4 * J, 4, 4)
            blocks = [(b_ * B, B, False) for b_ in range(NBLK)] + [(TP, TS, True)]
            for (t0, Bn, is_s) in blocks:
                E_r, E_i, R_ = (Ers, Eis, R0s) if is_s else (Er, Ei, R0)
                n = 4 * Bn
                if t0 == 0:
                    ms(rk, 0.0)
                if is_s:
                    h_r = sub(h0re, l * 64 + 4 * J, 4, 4); h_i = sub(h0im, l * 64 + 4 * J, 4, 4)
                    cmul(kcr, kci, h_r, h_i, sub(c1, 4 * J, 4, 4), sub(s1, 4 * J, 4, 4), sub(t1, 0, 4, 4), sub(t2, 0, 4, 4))
                    tt(rkr, kcr, rcol, ALU.mult); tt(rki, kci, rcol, ALU.mult)
                pre = bank(0, n)
                pim = bank(2, n)
                for jj in range(4):
                    a, slot = jj // 2, jj % 2
                    for (LBx, pp) in ((LBre, pre), (LBim, pim)):
                        pe('matmul', pp.ap[:, jj * Bn:(jj + 1) * Bn], LBx.ap[64 * a:64 * a + 64, slot * 128:(slot + 1) * 128],
                           ubf.ap[64 * a:64 * a + 64, t0:t0 + Bn], start=True, stop=True, r=[LBx, ubf], w=[pp])
                P3r = pre.ap.rearrange("p (a b) -> p a b", a=4)
                P3i = pim.ap.rearrange("p (a b) -> p a b", a=4)
                e3r = v3(E_r, Bn, Bn + 1); e3i = v3(E_i, Bn, Bn + 1)
                T1 = sub(t1, 0, n, 4); T2 = sub(t2, 0, n, 4)
                BR = sub(btr, 0, n, 4); BI = sub(bti, 0, n, 4)
                ttv(T1, v3(t1, Bn), pre, P3r, E_r, e3r, ALU.mult)
                ttv(T2, v3(t2, Bn), pim, P3i, E_i, e3i, ALU.mult)
                tt(BR, T1, T2, ALU.add)
                ttv(T1, v3(t1, Bn), pim, P3i, E_r, e3r, ALU.mult)
                ttv(T2, v3(t2, Bn), pre, P3r, E_i, e3i, ALU.mult)
                tt(BI, T1, T2, ALU.subtract)
                b0r = v3(btr, Bn)[:, :, 0]; b0i = v3(bti, Bn)[:, :, 0]
                dve('tensor_tensor', b0r, b0r, rkr.ap, ALU.add, r=[BR, rkr], w=[BR])
                dve('tensor_tensor', b0i, b0i, rki.ap, ALU.add, r=[BI, rki], w=[BI])
                KR = sub(kre, 0, n, 4); KI = sub(kim, 0, n, 4); RR = sub(R_, 0, n, 4)
                dve('tensor_tensor_scan', KR.ap, RR.ap, BR.ap, 0.0, ALU.mult, ALU.add, r=[RR, BR], w=[KR])
                dve('tensor_tensor_scan', KI.ap, RR.ap, BI.ap, 0.0, ALU.mult, ALU.add, r=[RR, BI], w=[KI])
                rot_out(kre, kim, E_r, E_i, Bn, hre, nhim, t1, t2)
                py = bank(4, Bn)
                c_matmul(LCre, LCim, hre, nhim, Bn, py)
                yo = sub(yt, t0, Bn, 4); ui = sub(uf, t0, Bn, 4)
                stt(yo, ui, dc.ap, py, ALU.mult, ALU.add, extra_r=[dc])
                carry_rot(kcr, kci, kre, kim, E_r, E_i, Bn, t1, t2)
                tt(rkr, kcr, rcol, ALU.mult); tt(rki, kci, rcol, ALU.mult)
                if (not is_s) and t0 + Bn == TP:
                    cp(sub(kce, 4 * J, 4, 4), kcr)
                    cp(sub(kce, 64 + 4 * J, 4, 4), kci)
                if is_s:
                    osr = sub(ostage, 128 + 4 * J, 4, 4); osi = sub(ostage, 192 + 4 * J, 4, 4)
                    cj = sub(c1, 4 * J, 4, 4); sj = sub(s1, 4 * J, 4, 4)
                    T1 = sub(t1, 0, 4, 4); T2 = sub(t2, 0, 4, 4)
                    tt(T1, kcr, cj, ALU.mult); tt(T2, kci, sj, ALU.mult); tt(osr, T1, T2, ALU.add)
                    tt(T1, kci, cj, ALU.mult); tt(T2, kcr, sj, ALU.mult); tt(osi, T1, T2, ALU.subtract)
            store(YLOC[J * 128:(J + 1) * 128, :], yt, key=('YLOC', J))
        AR.release()
        store(o_s5s[:, l * 128:(l + 1) * 128], sub(ostage, 128, 128, 4))
        gkeys = []
        store(GIN[l][:, NH * 128 + NH: NH * 128 + NH + 128], kce, key=('GIN', l, 'k'))
        gkeys.append(('GIN', l, 'k'))

        if kstop < 3:
            AR.marks = []
            break
        AR.mark()
        Sst = AR.alloc(128 * 4); Sbf = AR.alloc(128 * 2, BF16)
        att = AR.alloc(64 * 2, BF16); kT = AR.alloc(128 * 2, BF16); vT = AR.alloc(128 * 2, BF16)
        tmpS = AR.alloc(128 * 4)
        Gi = AR.alloc(NCHK * 4); Gi2 = AR.alloc(NCHK * 4); Ge = AR.alloc(NCHK * 4); eG = AR.alloc(NCHK * 4)
        qh = AR.alloc(TP * 2, BF16)
        s0t = AR.alloc(128 * 4)
        for hd in range(NH):
            qf, zf, vf, sgf, bc, ek = ft[0], ft[1], ft[2], ft[3], ft[4], ft[5]
            r0 = S5W + hd * 128
            load(qf, PROJ[r0:r0 + 128, :], key=('PROJ', r0 // 128))
            load(zf, PROJ[r0 + HGW:r0 + HGW + 128, :], key=('PROJ', (r0 + HGW) // 128))
            load(vf, PROJ[r0 + 2 * HGW:r0 + 2 * HGW + 128, :], key=('PROJ', (r0 + 2 * HGW) // 128))
            lbc = sub(lbs, l * 16 + hd, 1, 4); omc = sub(oml, l * 16 + hd, 1, 4); nomc = sub(noml, l * 16 + hd, 1, 4)
            actf(qf, qf, AF.Silu)
            actf(sgf, zf, AF.Sigmoid)
            ts(zf, sgf, omc.ap, lbc.ap, ALU.mult, ALU.add, extra_r=[omc, lbc])
            ts(zf, zf, TINY, None, ALU.max)
            actf(zf, zf, AF.Ln)
            ts(sgf, sgf, nomc.ap, omc.ap, ALU.mult, ALU.add, extra_r=[nomc, omc])
            dve('tensor_tensor_scan', bc.ap, M01.ap, zf.ap, 0.0, ALU.mult, ALU.add, r=[M01, zf], w=[bc])
            eq = zf
            actf(eq, bc, AF.Exp)
            actf(ek, bc, AF.Exp, scale=-1.0)
            qt, kt_, vb = bt[0], bt[1], bt[2]
            tt(qt, qf, eq, ALU.mult)
            tt(kt_, sgf, ek, ALU.mult)
            actf(vb, vf, AF.Copy)
            ends = bc.ap[:, :TP].rearrange("p (n c) -> p n c", c=64)[:, :, 63]
            dve('tensor_copy', Gi.ap, ends, r=[bc], w=[Gi])
            dve('tensor_tensor_scan', Gi2.ap, ONES(NCHK).ap, Gi.ap, 0.0, ALU.mult, ALU.add, r=[cst, Gi], w=[Gi2])
            ms(sub(Ge, 0, 1, 4), 0.0)
            if NCHK > 1:
                cp(sub(Ge, 1, NCHK - 1, 4), sub(Gi2, 0, NCHK - 1, 4))
            actf(eG, Ge, AF.Exp)
            dl = sub(Dloc, hd, 1, 4)
            actf(dl, sub(Gi2, NCHK - 1, 1, 4), AF.Exp)
            dve('tensor_tensor', qh.ap.rearrange("p (n c) -> p n c", c=64),
                qt.ap[:, :TP].rearrange("p (n c) -> p n c", c=64),
                eG.ap.unsqueeze(2).to_broadcast([128, NCHK, 64]), ALU.mult, r=[qt, eG], w=[qh])
            store(QHAT[hd * 128:(hd + 1) * 128, :], qh, key=('QHAT', hd))
            ol = ft[0]
            chunks = [(n_ * 64, 64, False) for n_ in range(NCHK)] + [(TP, TS, True)]
            ms(Sst, 0.0)
            ms(Sbf, 0.0)
            for (c0_, C, is_s) in chunks:
                if is_s:
                    store(GIN[l][:, hd * 128:(hd + 1) * 128], Sst, key=('GIN', l, hd))
                    gkeys.append(('GIN', l, hd))
                    load(s0t, dr['s0_hg'][:, (l * NH + hd) * 128:(l * NH + hd + 1) * 128])
                    cp(Sst, s0t)
                    cp(Sbf, s0t)
                pa = bank(0, 64); po = bank(1, 64); pkv = bank(2, 128)
                pk = bankT(0, 128); pv = bankT(128, 128)
                pe('matmul', pa.ap[:C, :C], kt_.ap[:, c0_:c0_ + C], qt.ap[:, c0_:c0_ + C], start=True, stop=True, r=[kt_, qt], w=[pa])
                pe('transpose', pk.ap[:C, :], kt_.ap[:, c0_:c0_ + C], IDB.ap, r=[kt_, IDB], w=[pk])
                pe('transpose', pv.ap[:C, :], vb.ap[:, c0_:c0_ + C], IDB.ap, r=[vb, IDB], w=[pv])
                dve('tensor_tensor', att.ap[:C, :C], pa.ap[:C, :C], MSKB.ap[:C, :C], ALU.mult, r=[pa, MSKB], w=[att])
                act('activation', kT.ap[:C, :], pk.ap[:C, :], AF.Copy, r=[pk], w=[kT])
                dve('tensor_copy', vT.ap[:C, :], pv.ap[:C, :], r=[pv], w=[vT])
                pe('matmul', po.ap[:, :C], vT.ap[:C, :], att.ap[:C, :C], start=True, stop=False, r=[vT, att], w=[po], signal=False)
                pe('matmul', po.ap[:, :C], Sbf.ap, qt.ap[:, c0_:c0_ + C], start=False, stop=True, r=[Sbf, qt], w=[po])
                pe('matmul', pkv.ap, kT.ap[:C, :], vT.ap[:C, :], start=True, stop=True, r=[kT, vT], w=[pkv])
                oo = sub(ol, c0_, C, 4)
                act('activation', oo.ap, po.ap[:, :C], AF.Copy, r=[po], w=[oo])
                ebc = sub(eq, c0_ + C - 1, 1, 4)
                ts(tmpS, pkv, ebc.ap, None, ALU.mult, extra_r=[ebc])
                stt(Sst, Sst, ebc.ap, tmpS, ALU.mult, ALU.add, extra_r=[ebc])
                cp(Sbf, Sst)
            store(o_hgs[:, (l * NH + hd) * 128:(l * NH + hd + 1) * 128], Sst)
            store(OLOC[hd * 128:(hd + 1) * 128, :], ol, key=('OLOC', hd))
        store(GIN[l][:, NH * 128:NH * 128 + NH], Dloc, key=('GIN', l, 'd'))
        gkeys.append(('GIN', l, 'd'))
        AR.release()

        if kstop < 4:
            AR.marks = []
            break
        USE_CC = True
        if USE_CC:
            pdeps = [('d', ('pool', k_), S.dcnt['pool'][k_]) for k_ in range(S.ndma['pool']) if S.dcnt['pool'][k_] > 0]
            S._emit_waits('pool', pdeps)
            S.custom('pool', lambda e, gi=GINt[l].ap().opt(), go=GOUTt[l].ap().opt(), sem=ccs[l]: e.collective_compute(
                "AllGather", ALU.bypass, replica_groups=[list(range(NCORE))], ins=[gi], outs=[go]).then_inc(sem),
                deps=[], reads=gkeys)
            S.prog['pool'].append(lambda e, sem=ccs[l]: e.wait_ge(sem, 1))
            S.prog['sp'].append(lambda e, sem=ccs[l]: e.wait_ge(sem, 1))
        GO = GOUT[l]
        AR.mark()
        Sin = AR.alloc(NH * 128 * 4)
        Sinb = AR.alloc(NH * 128 * 2, BF16)
        gS = [AR.alloc(NH * 128 * 4) for _ in range(2)]
        gD = AR.alloc(NCORE * NH * 4)
        gK = AR.alloc(NCORE * 128 * 4)
        Aj = AR.alloc(NH * 4)
        kn = AR.alloc(128 * 4); kt1 = AR.alloc(64 * 4); kt2 = AR.alloc(64 * 4); kdf = AR.alloc(128 * 4)
        if USE_CC:
            for r in range(NCORE):
                load(sub(gD, r * NH, NH, 4), GO[r * 128:(r + 1) * 128, NH * 128:NH * 128 + NH])
                load(sub(gK, r * 128, 128, 4), GO[r * 128:(r + 1) * 128, NH * 128 + NH:NH * 128 + NH + 128])
        else:
            ms(gD, 0.0)
            ms(gK, 0.0)
            ms(gS[0], 0.0)
            ms(gS[1], 0.0)
        ms(Sin, 0.0)
        ms(kcin, 0.0)
        nr = sub(kn, 0, 64, 4); ni = sub(kn, 64, 64, 4)
        for r in range(NCORE - 1):
            g_ = gS[r % 2]
            if USE_CC:
                load(g_, GO[r * 128:(r + 1) * 128, 0:NH * 128])
            mr = sub(cmask, r, 1, 4)
            Dr = sub(gD, r * NH, NH, 4)
            ts(Aj, Dr, -1.0, None, ALU.add)
            ts(Aj, Aj, mr.ap, None, ALU.mult, extra_r=[mr])
            ts(Aj, Aj, 1.0, None, ALU.add)
            S3 = Sin.ap.rearrange("p (h v) -> p h v", h=NH)
            dve('tensor_tensor', S3, S3, Aj.ap.unsqueeze(2).to_broadcast([128, NH, 128]), ALU.mult, r=[Sin, Aj], w=[Sin])
            stt(Sin, g_, mr.ap, Sin, ALU.mult, ALU.add, extra_r=[mr])
            gr = sub(gK, r * 128, 64, 4); gi_ = sub(gK, r * 128 + 64, 64, 4)
            cmul(nr, ni, lre, lim, kir, kii, kt1, kt2)
            tt(nr, nr, gr, ALU.add); tt(ni, ni, gi_, ALU.add)
            tt(kdf, kn, kcin, ALU.subtract)
            stt(kcin, kdf, mr.ap, kcin, ALU.mult, ALU.add, extra_r=[mr])
        cp(Sinb, Sin)
        for hd in range(NH):
            g_ = sub(gS[0], hd * 128, 128, 4)
            load(g_, GIN[l][:, hd * 128:(hd + 1) * 128], key=('GIN', l, hd))
            dl = sub(Dloc, hd, 1, 4)
            si = sub(Sin, hd * 128, 128, 4)
            stt(g_, si, dl.ap, g_, ALU.mult, ALU.add, extra_r=[dl])
        store(o_hgp[:, l * NH * 128:(l + 1) * NH * 128], gS[0])
        cmul(nr, ni, lre, lim, kir, kii, kt1, kt2)
        tt(kn, kn, kce, ALU.add)
        osr = sub(ostage, 0, 64, 4); osi = sub(ostage, 64, 64, 4)
        tt(kt1, nr, c1, ALU.mult); tt(kt2, ni, s1, ALU.mult); tt(osr, kt1, kt2, ALU.add)
        tt(kt1, ni, c1, ALU.mult); tt(kt2, nr, s1, ALU.mult); tt(osi, kt1, kt2, ALU.subtract)
        store(o_s5p[:, l * 128:(l + 1) * 128], sub(ostage, 0, 128, 4))

        if kstop < 5:
            AR.marks = []
            break
        for hd in range(NH):
            of = ft[hd % 2]
            gf = ft[2 + hd % 2]
            qhb = bt[2 + hd % 2]
            load(of, OLOC[hd * 128:(hd + 1) * 128, :], key=('OLOC', hd))
            load(sub(qhb, 0, TP, 2), QHAT[hd * 128:(hd + 1) * 128, :], key=('QHAT', hd))
            r0 = S5W + 3 * HGW + hd * 128
            load(gf, PROJ[r0:r0 + 128, :], key=('PROJ', r0 // 128))
            sib = sub(Sinb, hd * 128, 128, 2)
            t0 = 0
            bi = 0
            while t0 < TP:
                sz = min(512, TP - t0)
                pf = bank(bi % 2, sz)
                pe('matmul', pf.ap, sib.ap, qhb.ap[:, t0:t0 + sz], start=True, stop=True, r=[sib, qhb], w=[pf])
                oo = sub(of, t0, sz, 4)
                tt(oo, oo, pf, ALU.add)
                t0 += sz
                bi += 1
            stats_accum([of], 1.0 / 128)
            actf(gf, gf, AF.Silu)
            tt(of, of, rstd, ALU.mult)
            o_ = sub(actT, (16 + hd) * N, N, 2)
            gh = sub(g_hg, l, 1, 4)
            stt(o_, of, gh.ap, gf, ALU.mult, ALU.mult, extra_r=[gh])
        AR.release()

        if kstop < 6:
            AR.marks = []
            break
        AR.mark()
        Er = AR.alloc(W1 * 4); Ei = AR.alloc(W1 * 4); Rp = AR.alloc(4 * B * 4)
        tmpA = AR.alloc(2 * W1 * 4)
        t1 = AR.alloc(4 * B * 4); t2 = AR.alloc(4 * B * 4)
        kre = AR.alloc(4 * B * 4); kim = AR.alloc(4 * B * 4)
        hre = AR.alloc(4 * B * 2, BF16); nhim = AR.alloc(4 * B * 2, BF16)
        LC = AR.alloc(256 * 2 * 2, BF16)
        LCf = AR.alloc(512 * 4)
        kc = AR.alloc(8 * 4); kc2 = AR.alloc(8 * 4)
        LCre = sub(LC, 0, 256, 2); LCim = sub(LC, 256, 256, 2)
        kcr = sub(kc, 0, 4, 4); kci = sub(kc, 4, 4, 4)
        k2r = sub(kc2, 0, 4, 4); k2i = sub(kc2, 4, 4, 4)
        for J in range(16):
            c0 = l * 4096 + J * 256
            load(sub(LCf, 0, 256, 4), dr['lc_re'][:, c0:c0 + 256]); load(sub(LCf, 256, 256, 4), dr['lc_im'][:, c0:c0 + 256])
            S.op('pool', 'tensor_copy', (LC.ap, LCf.ap), {}, [LCf], [LC])
            s5_tables(J, B, Er, Ei, Rp, tmpA, 'rpow')
            yt = ft[J % 2]
            load(yt, YLOC[J * 128:(J + 1) * 128, :], key=('YLOC', J))
            cp(kcr, sub(kcin, 4 * J, 4, 4))
            cp(kci, sub(kcin, 64 + 4 * J, 4, 4))
            for b_ in range(NBLK):
                t0 = b_ * B
                n = 4 * B
                KR = sub(kre, 0, n, 4); KI = sub(kim, 0, n, 4)
                dve('tensor_tensor', v3(kre, B), v3(Rp, B), kcr.ap.unsqueeze(2).to_broadcast([128, 4, B]), ALU.mult, r=[Rp, kcr], w=[KR])
                dve('tensor_tensor', v3(kim, B), v3(Rp, B), kci.ap.unsqueeze(2).to_broadcast([128, 4, B]), ALU.mult, r=[Rp, kci], w=[KI])
                rot_out(kre, kim, Er, Ei, B, hre, nhim, t1, t2)
                py = bank(4, B)
                c_matmul(LCre, LCim, hre, nhim, B, py)
                yo = sub(yt, t0, B, 4)
                tt(yo, yo, py, ALU.add)
                if b_ < NBLK - 1:
                    carry_rot(k2r, k2i, kre, kim, Er, Ei, B, t1, t2)
                    cp(kc, kc2)
            zo = sub(actT, J * N, N, 2)
            actf(zo, yt, AF.Gelu)
        AR.release()
        AR.release()

        if kstop < 7:
            AR.marks = []
            break
        zz = act2

        def evac_glu(mi, banks):
            sg_ = ft[4 + ev_ctr[0] % 2]
            ev_ctr[0] += 1
            for bi, (t0, sz) in enumerate(TB):
                actf(sub(sg_, t0, sz, 4), banks[bi], AF.Sigmoid)
            tt(sub(zz, mi * N, N, 2), sub(actT, mi * N, N, 2), sg_, ALU.mult)
        dense('w_glu', l * 8 * 128, 16, 8, actT, evac_glu)
        stats_accum([sub(zz, i * N, N, 2) for i in range(16)], 1.0 / S5W)
        for i in range(16):
            gc = sub(g_s5, l * 16 + i, 1, 4)
            stt(sub(actT, i * N, N, 2), sub(zz, i * N, N, 2), gc.ap, rstd, ALU.mult, ALU.mult, extra_r=[gc])

        if kstop < 8:
            AR.marks = []
            break
        dense('w_out', l * 16 * 128, 32, 16, actT, evac_resid)

        if kstop < 9:
            AR.marks = []
            break
        x_norm_to_act(g_ffn, l * 32)
        p0 = 0
        for ci, npan in enumerate(FFN_CH):
            def evac_gate(mi, banks):
                for bi, (t0, sz) in enumerate(TB):
                    actf(sub(act2, mi * N + t0, sz, 2), banks[bi], AF.Silu)

            def evac_up(mi, banks):
                for bi, (t0, sz) in enumerate(TB):
                    oo = sub(act2, mi * N + t0, sz, 2)
                    tt(oo, oo, banks[bi], ALU.mult)
            dense('w_gate', (l * 43 + p0) * 128, 32, npan, actT, evac_gate)
            dense('w_up', (l * 43 + p0) * 128, 32, npan, actT, evac_up)
            dense('w_down%d' % ci, l * 16 * 128, 2 * npan, 16, act2, evac_resid)
            p0 += npan

    stats_stream(32, lambda i: ldX(i, 0), 1.0 / D)
    for i in range(32):
        b_ = ldX(i, 2)
        gc = sub(g_fin, i, 1, 4)
        o_ = ft[4 + i % 2]
        stt(o_, b_, gc.ap, rstd, ALU.mult, ALU.mult, extra_r=[gc])
        store(yT[i * 128:(i + 1) * 128, :], o_)
    S.finish()
    with nc.Block() as block:
        S.run(block)
    return nc, S


def run_model(inp, depth, TP):
    sh, per = build_host_inputs(inp, depth, TP)
    shapes = {k: v.shape for k, v in sh.items()}
    shapes.update({k: v.shape for k, v in per[0].items()})
    import os
    nc, S = build_program(depth, TP, shapes, int(os.environ.get('KSTOP', '99')))
    in_maps = []
    for c in range(NCORE):
        m = dict(sh)
        m.update(per[c])
        in_maps.append(m)
    res = run_bass_kernel_spmd(nc, in_maps, core_ids=list(range(NCORE)))
    return assemble(res.results, depth, TP)


def assemble(R, depth, TP):
    N = TP + TS
    f = np.float32
    yp = np.concatenate([R[c]['yT'][:, :TP].T for c in range(NCORE)], 0)[None]
    ys = np.stack([R[c]['yT'][:, TP:].T for c in range(NCORE)], 0)

    def s5_unlay(a):
        a4 = a.reshape(2, 64, depth, 2, 64)
        b = a4.transpose(2, 3, 4, 0, 1).reshape(depth, 2, 128, 64)
        return b[:, 0], b[:, 1]

    def hg_unlay(a):
        return a.reshape(128, depth, NH, 128).transpose(1, 2, 0, 3)
    pr, pi_ = s5_unlay(R[NCORE - 1]['o_s5p'])
    p_re = pr[:, None].astype(f); p_im = pi_[:, None].astype(f)
    p_hg = hg_unlay(R[NCORE - 1]['o_hgp'])[:, None].astype(f)
    srs = [s5_unlay(R[c]['o_s5s']) for c in range(NCORE)]
    s_re = np.stack([x[0] for x in srs], 1).astype(f)
    s_im = np.stack([x[1] for x in srs], 1).astype(f)
    s_hg = np.stack([hg_unlay(R[c]['o_hgs']) for c in range(NCORE)], 1).astype(f)
    return (np.ascontiguousarray(yp, f), np.ascontiguousarray(ys, f), np.ascontiguousarray(p_re), np.ascontiguousarray(p_im),
            np.ascontiguousarray(p_hg), np.ascontiguousarray(s_re), np.ascontiguousarray(s_im), np.ascontiguousarray(s_hg))


def kernel(**inputs):
    return run_model(inputs, 4, 1024)
```
